# Optimizing a Trainium2 kernel written in Bass

```python
import math
import jax, jax.numpy as jnp
from jax import lax
import numpy as np

D_MODEL = 2048
BATCH = 4
SEQ = 8192
DEPTH = 2

N_MIXERS = 2
N_S5 = (DEPTH + 1) // 2
N_CONV = DEPTH // 2
S5_GROUP = 16
S5_GROUPS = D_MODEL // S5_GROUP
S5_STATE = 64
S5_CHUNK = 128
DT_MIN = 1e-3
DT_MAX = 1e-1
CONV_WIDTH = 31
D_FF = ((8 * D_MODEL + 3 * 256 - 1) // (3 * 256)) * 256
ALPHA = (2 * DEPTH) ** 0.25
BETA = (8 * DEPTH) ** -0.25
LN_EPS = 1e-5
N_MOD = 6

kernel_name = 'hybrid_s5_conformer_conv_deepnorm_adaln'


def layer_norm(x, g, b):
    xf = x.astype(jnp.float32)
    mu = jnp.mean(xf, axis=-1, keepdims=True)
    xc = xf - mu
    var = jnp.mean(xc * xc, axis=-1, keepdims=True)
    return (xc * lax.rsqrt(var + LN_EPS) * g.astype(jnp.float32) + b.astype(jnp.float32)).astype(x.dtype)


def _ssm_combine(e1, e2):
    a1, b1 = e1
    a2, b2 = e2
    return a1 * a2, a2 * b1 + b2


def s5_mixer(u, lam_re, lam_im, log_dt, b_re, b_im, c_re, c_im, d_skip, w_glu):
    f32 = jnp.float32
    bsz, seq, d = u.shape
    lam = lax.complex(lam_re.astype(f32), lam_im.astype(f32))
    dt = jnp.exp(log_dt.astype(f32))[:, None]
    a_bar = jnp.exp(lam * dt)
    b_mat = lax.complex(b_re.astype(f32), b_im.astype(f32))
    b_bar = ((a_bar - 1.0) / lam)[..., None] * b_mat
    c_mat = lax.complex(c_re.astype(f32), c_im.astype(f32))
    steps = jnp.arange(1, S5_CHUNK + 1, dtype=f32)[:, None, None]
    a_pow = jnp.exp(lam * dt * steps)
    a_elems = jnp.broadcast_to(a_bar, (bsz, S5_CHUNK, S5_GROUPS, S5_STATE))

    n_chunks = seq // S5_CHUNK
    u_chunks = u.astype(f32).reshape(bsz, n_chunks, S5_CHUNK, S5_GROUPS, S5_GROUP)
    u_chunks = jnp.transpose(u_chunks, (1, 0, 2, 3, 4))

    def chunk_step(h0, u_c):
        bu = jnp.einsum('btgi,gpi->btgp', u_c.astype(jnp.complex64), b_bar)
        _, h_local = lax.associative_scan(_ssm_combine, (a_elems, bu), axis=1)
        h = h_local + a_pow[None] * h0[:, None]
        y_c = jnp.real(jnp.einsum('btgp,gip->btgi', h, c_mat))
        return h[:, -1], y_c

    h_init = jnp.zeros((bsz, S5_GROUPS, S5_STATE), jnp.complex64)
    _, y = lax.scan(chunk_step, h_init, u_chunks)
    y = jnp.transpose(y, (1, 0, 2, 3, 4)).reshape(bsz, seq, d)
    y = jax.nn.gelu(y + d_skip.astype(f32) * u.astype(f32)).astype(u.dtype)
    vg = y @ w_glu
    return vg[..., :d] * jax.nn.sigmoid(vg[..., d:])


def conv_mixer(u, w_pw1, w_dw, b_dw, g_norm, b_norm, w_pw2):
    d = u.shape[-1]
    ab = u @ w_pw1
    v = ab[..., :d] * jax.nn.sigmoid(ab[..., d:])
    v = lax.conv_general_dilated(
        v, w_dw[:, None, :], window_strides=(1,),
        padding=((CONV_WIDTH - 1, 0),),
        dimension_numbers=('NWC', 'WIO', 'NWC'),
        feature_group_count=d) + b_dw
    v = layer_norm(v, g_norm, b_norm)
    return jax.nn.silu(v) @ w_pw2


def swiglu_ffn(u, w_gu, w_down):
    gu = u @ w_gu
    g, v = jnp.split(gu, 2, axis=-1)
    return (jax.nn.silu(g) * v) @ w_down


def setup_inputs(seed: int = 0) -> dict:
    key = jax.random.key(seed)
    ks = jax.random.split(key, 24)
    f32 = jnp.float32
    D, G, P, I, F, W = D_MODEL, S5_GROUPS, S5_STATE, S5_GROUP, D_FF, CONV_WIDTH
    nrm = lambda k, shape, s: jax.random.normal(k, shape, f32) * s
    n_idx = jnp.arange(P, dtype=f32)
    return {
        'x': nrm(ks[0], (BATCH, SEQ, D), 1.0),
        'c': nrm(ks[1], (BATCH, D), 1.0),
        'ada_w': nrm(ks[2], (DEPTH, D, N_MOD * D), 0.1 * D ** -0.5),
        'ada_b': nrm(ks[3], (DEPTH, N_MOD * D), 0.01),
        'ln_g': 1.0 + nrm(ks[4], (DEPTH, 2, D), 0.01),
        'ln_b': nrm(ks[5], (DEPTH, 2, D), 0.01),
        's5_lam_re': -0.5 + nrm(ks[6], (N_S5, G, P), 0.01),
        's5_lam_im': math.pi * n_idx + nrm(ks[7], (N_S5, G, P), 0.01),
        's5_log_dt': jax.random.uniform(ks[8], (N_S5, G), f32, math.log(DT_MIN), math.log(DT_MAX)),
        's5_b_re': nrm(ks[9], (N_S5, G, P, I), (2 * I) ** -0.5),
        's5_b_im': nrm(ks[10], (N_S5, G, P, I), (2 * I) ** -0.5),
        's5_c_re': nrm(ks[11], (N_S5, G, I, P), (2 * P) ** -0.5),
        's5_c_im': nrm(ks[12], (N_S5, G, I, P), (2 * P) ** -0.5),
        's5_d': nrm(ks[13], (N_S5, D), 1.0),
        's5_w_glu': nrm(ks[14], (N_S5, D, 2 * D), BETA * D ** -0.5),
        'cv_w_pw1': nrm(ks[15], (N_CONV, D, 2 * D), D ** -0.5),
        'cv_w_dw': nrm(ks[16], (N_CONV, W, D), W ** -0.5),
        'cv_b_dw': nrm(ks[17], (N_CONV, D), 0.01),
        'cv_norm_g': 1.0 + nrm(ks[18], (N_CONV, D), 0.01),
        'cv_norm_b': nrm(ks[19], (N_CONV, D), 0.01),
        'cv_w_pw2': nrm(ks[20], (N_CONV, D, D), BETA * D ** -0.5),
        'ffn_w_gu': nrm(ks[21], (DEPTH, D, 2 * F), D ** -0.5),
        'ffn_w_down': nrm(ks[22], (DEPTH, F, D), BETA * F ** -0.5),
    }


def reference(x, c, ada_w, ada_b, ln_g, ln_b, s5_lam_re, s5_lam_im, s5_log_dt, s5_b_re, s5_b_im,
              s5_c_re, s5_c_im, s5_d, s5_w_glu, cv_w_pw1, cv_w_dw, cv_b_dw, cv_norm_g, cv_norm_b,
              cv_w_pw2, ffn_w_gu, ffn_w_down):
    c_act = jax.nn.silu(c)
    for i in range(DEPTH):
        mod = c_act @ ada_w[i] + ada_b[i]
        sh_m, sc_m, gt_m, sh_f, sc_f, gt_f = [m[:, None, :] for m in jnp.split(mod, N_MOD, axis=-1)]
        j = i // N_MIXERS
        h = x * (1.0 + sc_m) + sh_m
        if i % N_MIXERS == 0:
            y = s5_mixer(h, s5_lam_re[j], s5_lam_im[j], s5_log_dt[j], s5_b_re[j], s5_b_im[j],
                         s5_c_re[j], s5_c_im[j], s5_d[j], s5_w_glu[j])
        else:
            y = conv_mixer(h, cv_w_pw1[j], cv_w_dw[j], cv_b_dw[j], cv_norm_g[j], cv_norm_b[j], cv_w_pw2[j])
        x = layer_norm(ALPHA * x + (1.0 + gt_m) * y, ln_g[i, 0], ln_b[i, 0])
        h = x * (1.0 + sc_f) + sh_f
        y = swiglu_ffn(h, ffn_w_gu[i], ffn_w_down[i])
        x = layer_norm(ALPHA * x + (1.0 + gt_f) * y, ln_g[i, 1], ln_b[i, 1])
    return x
```

```python
import math
import contextlib
import numpy as np
import concourse.bass as bass
import concourse.mybir as mybir
from concourse.bass_utils import run_bass_kernel_spmd

F32 = mybir.dt.float32
BF16 = mybir.dt.bfloat16
U8 = mybir.dt.uint8
ALU = mybir.AluOpType
AF = mybir.ActivationFunctionType

D = 2048
NCH = 16
DFF = 5632
NFC = 44
G = 128
SEQH = 4096
NT = 512
NH = 128
CW = 31
ALPHA = 4.0 ** 0.25
LN_EPS = 1e-5
TWO_PI = 2.0 * math.pi
CW1 = 6.28125
CW2 = float(np.float32(TWO_PI - CW1))
CW3 = float(TWO_PI - CW1 - CW2)
MAGIC = 12582912.0
SIN_SCALE = 1.0 - 2e-5
ENGS = ("pe", "act", "dve", "pool", "sp")
LAST_INPUTS = []


class Ins:
    __slots__ = ("eng", "fn", "waits", "inc", "val", "idx", "dma_sem", "dma_val")

    def __init__(self, eng, fn):
        self.eng = eng
        self.fn = fn
        self.waits = []
        self.inc = False
        self.val = 0
        self.idx = 0
        self.dma_sem = None
        self.dma_val = 0


class Prog:
    def __init__(self, nc):
        self.nc = nc
        self.streams = {e: [] for e in ENGS}
        self.last_w = {}
        self.readers = {}
        self.waited = {e: {} for e in ENGS}
        self.dma_count = {}
        self.last_dma = {}
        self.n_ins = 0

    def _dep(self, ins, dep):
        if dep is None or dep is ins:
            return
        if dep.dma_sem is None:
            if dep.eng == ins.eng and dep.eng == "pe":
                return
            key = dep.eng
            idx = dep.idx
        else:
            key = ("dma", dep.dma_sem)
            idx = dep.dma_val
        w = self.waited[ins.eng]
        if w.get(key, -1) >= idx:
            return
        w[key] = idx
        ins.waits.append(dep)
        dep.inc = True

    def op(self, eng, fn, reads=(), writes=(), dma_sem=None, extra=()):
        ins = Ins(eng, fn)
        st = self.streams[eng]
        ins.idx = len(st)
        if dma_sem is not None:
            ins.dma_sem = dma_sem
            self.dma_count[dma_sem] = self.dma_count.get(dma_sem, 0) + 16
            ins.dma_val = self.dma_count[dma_sem]
            self.last_dma[dma_sem] = ins
        for d in extra:
            self._dep(ins, d)
        for r in reads:
            self._dep(ins, self.last_w.get(r))
            if r.startswith("PS"):
                for rd in self.readers.get(r, ()):
                    if rd.eng != eng:
                        self._dep(ins, rd)
        for r in writes:
            self._dep(ins, self.last_w.get(r))
            for rd in self.readers.get(r, ()):
                self._dep(ins, rd)
        for r in reads:
            lst = self.readers.setdefault(r, [])
            if ins.dma_sem is None:
                lst[:] = [x for x in lst if not (x.dma_sem is None and x.eng == eng)]
            lst.append(ins)
        for r in writes:
            self.last_w[r] = ins
            self.readers[r] = []
        st.append(ins)
        self.n_ins += 1
        return ins

    def barrier(self):
        deps = []
        for e in ("pe", "act", "dve", "pool"):
            for ins in reversed(self.streams[e]):
                if ins.dma_sem is None and ins.fn is not None:
                    deps.append(ins)
                    break
        deps.extend(self.last_dma.values())
        for e in ("pe", "act", "dve", "pool", "sp"):
            self.op(e, None, extra=deps)
        self.last_w = {}
        self.readers = {}

    def emit(self, final_waits=()):
        nc = self.nc
        with contextlib.ExitStack() as es:
            esem = {}
            for e in ("pe", "act", "dve", "pool"):
                esem[e] = es.enter_context(nc.semaphore("s_" + e))
            dsem = {}
            for k in self.dma_count:
                dsem[k] = es.enter_context(nc.semaphore("d_%s" % (k,)))
            fin = Ins("sp", None)
            fin.idx = len(self.streams["sp"])
            for d in final_waits:
                self._dep(fin, d)
            for e in ("pe", "act", "dve", "pool"):
                c = 0
                for ins in self.streams[e]:
                    if ins.dma_sem is None:
                        if ins.inc and ins.fn is not None:
                            c += 1
                        ins.val = c
            block = es.enter_context(nc.Block())
            handles = {"pe": "tensor", "act": "scalar", "dve": "vector", "pool": "gpsimd", "sp": "sync"}

            def run_stream(e, eng):
                def wait_for(dep):
                    if dep.dma_sem is None:
                        eng.wait_ge(esem[dep.eng], dep.val)
                    else:
                        eng.wait_ge(dsem[dep.dma_sem], dep.dma_val)
                for ins in self.streams[e]:
                    for dep in ins.waits:
                        wait_for(dep)
                    if ins.fn is None:
                        continue
                    r = ins.fn(eng)
                    if ins.dma_sem is not None:
                        r.then_inc(dsem[ins.dma_sem], 16)
                    elif ins.inc:
                        r.then_inc(esem[e], 1)
                if e == "sp":
                    for dep in fin.waits:
                        wait_for(dep)

            for e in ENGS:
                def sec(eng, e=e):
                    run_stream(e, eng)
                getattr(block, handles[e])(sec)


class Arena:
    def __init__(self, nc, nbytes):
        self.t = nc.alloc_sbuf_tensor("arena", [128, nbytes], U8)
        self.nbytes = nbytes
        self.top = 0
        self.marks = []

    def alloc(self, shape, dt=F32):
        esz = 4 if dt == F32 else 2
        n = 1
        for s in shape[1:]:
            n *= s
        nb = n * esz
        nb_al = (nb + 63) // 64 * 64
        assert self.top + nb_al <= self.nbytes, "SBUF arena overflow: need %d have %d" % (self.top + nb_al, self.nbytes)
        v = self.t[0:shape[0], self.top:self.top + nb].bitcast(dt)
        self.top += nb_al
        if len(shape) == 3:
            v = v.rearrange("p (a b) -> p a b", a=shape[1])
        elif len(shape) == 4:
            v = v.rearrange("p (a b c) -> p a b c", a=shape[1], b=shape[2])
        return v

    def mark(self):
        return self.top

    def reset(self, m):
        self.top = m


class Builder:
    def __init__(self, cfg):
        self.cfg = cfg
        nc = bass.Bass("TRN2", target_bir_lowering=False)
        self.nc = nc
        self.p = Prog(nc)
        self.es = contextlib.ExitStack()
        self.dram = {}

    def act(self, out, in_, func, reads, writes, scale=1.0, bias=0.0):
        self.p.op("act", lambda e: e.activation(out=out, in_=in_, func=func, bias=bias, scale=scale), reads, writes)

    def tt(self, eng, out, in0, in1, op, reads, writes):
        self.p.op(eng, lambda e: e.tensor_tensor(out=out, in0=in0, in1=in1, op=op), reads, writes)

    def ts(self, eng, out, in0, s1, s2, op0, op1, reads, writes):
        if s2 is None:
            self.p.op(eng, lambda e: e.tensor_scalar(out=out, in0=in0, scalar1=s1, scalar2=None, op0=op0), reads, writes)
        else:
            self.p.op(eng, lambda e: e.tensor_scalar(out=out, in0=in0, scalar1=s1, scalar2=s2, op0=op0, op1=op1), reads, writes)

    def stt(self, out, in0, scalar, in1, op0, op1, reads, writes):
        self.p.op("dve", lambda e: e.scalar_tensor_tensor(out=out, in0=in0, scalar=scalar, in1=in1, op0=op0, op1=op1), reads, writes)

    def copy(self, eng, out, in_, reads, writes):
        if eng == "act":
            self.p.op("act", lambda e: e.activation(out=out, in_=in_, func=AF.Identity), reads, writes)
        else:
            self.p.op(eng, lambda e: e.tensor_copy(out=out, in_=in_), reads, writes)

    def mm(self, out, lhsT, rhs, start, stop, reads, writes):
        self.p.op("pe", lambda e: e.matmul(out, lhsT=lhsT, rhs=rhs, start=start, stop=stop), reads, writes)

    def tr(self, out, in_, ident, reads, writes):
        self.p.op("pe", lambda e: e.transpose(out=out, in_=in_, identity=ident), reads, writes)

    def dma(self, q, out, in_, sem, reads, writes, slow=False):
        if sem is None:
            sem = (list(writes) + list(reads))[0]
        if slow:
            return self.p.op(q, lambda e: e.dma_start(out=out, in_=in_, allow_slow_non_contiguous=True), reads, writes, dma_sem=sem)
        return self.p.op(q, lambda e: e.dma_start(out=out, in_=in_), reads, writes, dma_sem=sem)

    def memset(self, eng, ap, val, writes):
        self.p.op(eng, lambda e: e.memset(ap, val), (), writes)

    def din(self, name, shape, dt=F32):
        LAST_INPUTS.append(name)
        t = self.nc.dram_tensor(name, list(shape), dt, kind="ExternalInput").ap()
        self.dram[name] = t
        return t

    def dscr(self, name, shape, dt=F32, out=False):
        t = self.nc.dram_tensor(name, list(shape), dt, kind="ExternalOutput" if out else "Internal").ap()
        self.dram[name] = t
        return t

    def sincos(self, ang, kbuf, rbuf, sin_out, cos_out, k_ang, k_k, k_r, k_sin, k_cos):
        self.ts("dve", kbuf, ang, 1.0 / TWO_PI, MAGIC, ALU.mult, ALU.add, [k_ang], [k_k])
        self.ts("dve", kbuf, kbuf, -MAGIC, None, ALU.add, None, [k_k], [k_k])
        self.stt(rbuf, kbuf, -CW1, ang, ALU.mult, ALU.add, [k_ang, k_k], [k_r])
        self.stt(rbuf, kbuf, -CW2, rbuf, ALU.mult, ALU.add, [k_k, k_r], [k_r])
        self.stt(rbuf, kbuf, -CW3, rbuf, ALU.mult, ALU.add, [k_k, k_r], [k_r])
        self.act(sin_out, rbuf, AF.Sin, [k_r], [k_sin], scale=SIN_SCALE)
        self.act(kbuf, rbuf, AF.Abs, [k_r], [k_k])
        self.act(cos_out, kbuf, AF.Sin, [k_k], [k_cos], scale=-1.0, bias=self.HALFPI[:, 0:1])


def build(cfg):
    del LAST_INPUTS[:]
    b = Builder(cfg)
    nc, p = b.nc, b.p
    dbg = cfg.get("debug", False)
    x_own = b.din("x_own", [SEQH, D])
    x_prev = b.din("x_prev", [SEQH, D])
    c_fm = b.din("c_fm", [128, NCH])
    flag = b.din("flag", [128, 1])
    ada_small = cfg.get("ada_small", False)
    ada_w = b.din("ada_w", [2, D, 6 * D] if not ada_small else [1, D, 4096])
    ada_b = b.din("ada_b", [2, 6 * D])
    vecs = b.din("vecs", [128, 12, NCH])
    cw_fm = b.din("cw_fm", [128, NCH, CW])
    lr2 = b.din("lr2", [128, G]); li2 = b.din("li2", [128, G]); ldt = b.din("ldt", [128, G])
    bri = b.din("bri", [128, G, 16]); bir = b.din("bir", [128, G, 16])
    cri = b.din("cri", [128, G, 16]); cir = b.din("cir", [128, G, 16])
    sel_d = b.din("sel", [128, 8, 128]); ident_d = b.din("ident", [128, 128]); mask_d = b.din("mask", [128, 128]); psw_d = b.din("psw", [128, 128])
    nidx_d = b.din("nidx", [128, 512]); tau_d = b.din("tau", [128, 16, G])
    if not cfg.get("phaseA_only", False):
        w_glu = b.din("w_glu", [D, 2 * D]); w_pw1 = b.din("w_pw1", [D, 2 * D]); w_pw2 = b.din("w_pw2", [D, D])
        w_gu = b.din("w_gu", [2, D, 2 * DFF]); w_dn = b.din("w_dn", [2, DFF, D])
    out_d = b.dscr("out", [SEQH, D], out=True)
    mod_scr = b.dscr("mod_scr", [2, 6 * D], out=dbg)
    y_scr = b.dscr("y_scr", [NH + SEQH, D], out=dbg)

    arena = Arena(nc, 206 * 1024)
    A = arena.alloc
    es = b.es
    PS = [es.enter_context(nc.psum_tensor("ps%d" % i, [128, 512], F32))[:] for i in range(8)]

    IDENT = A([128, 128]); MASK = A([128, 128]); PSWM = A([128, 128]); FLAG = A([128, 1])
    HALFPI = A([128, 1]); b.HALFPI = HALFPI
    MAGP = A([128, 1]); MAGN = A([128, 1]); ONEP = A([128, 1])
    VECS = A([128, 12, NCH]); CWF = A([128, NCH, CW]); CACT = A([128, NCH])
    MODF = A([128, 2, 96])
    IDENTB = A([128, 128], BF16); ONESB = A([128, 128], BF16)
    for (dst, src, k) in [(IDENT, ident_d, "IDENT"), (MASK, mask_d, "MASK"), (PSWM, psw_d, "PSWM"), (FLAG, flag, "FLAG"),
                          (VECS, vecs, "VECS"), (CWF, cw_fm, "CWF"), (CACT, c_fm, "CACT")]:
        b.dma("sp", dst, src, None, [], [k])
    b.memset("pool", HALFPI, math.pi / 2, ["HALFPI"])
    b.memset("pool", MAGP, MAGIC, ["MAGP"])
    b.memset("pool", ONEP, 1.0, ["ONEP"])
    b.memset("pool", MAGN, -MAGIC, ["MAGN"])
    b.memset("pool", ONESB, 1.0 / D, ["ONESB"])
    b.copy("pool", IDENTB, IDENT, ["IDENT"], ["IDENTB"])
    b.act(CACT, CACT, AF.Silu, ["CACT"], ["CACT"])
    persist_mark = arena.mark()

    ADAS = [A([128, 1024]) for _ in range(2)]
    ADAB = [A([128, 1024]) for _ in range(2)]
    MB = [A([128, 1024]) for _ in range(2)]
    JUNK = A([128, 128])
    CACTB = A([128, NCH, 128])
    b.copy("pool", CACTB, CACT[:].unsqueeze(2).to_broadcast([128, NCH, 128]), ["CACT"], ["CACTB"])

    def ada_steps():
        cnt = 0
        for li in range(2 if not ada_small else 1):
            for cg in range(12 if not ada_small else 4):
                bb = cg % 2
                b.dma("sp", ADAB[bb], ada_b[li:li + 1, cg * 1024:(cg + 1) * 1024].to_broadcast([128, 1024]), "adab%d" % bb, [], ["ADAB%d" % bb])
                for kc in range(16):
                    s = cnt % 2
                    cnt += 1
                    b.dma("sp", ADAS[s], ada_w[li, kc * 128:(kc + 1) * 128, cg * 1024:(cg + 1) * 1024], "adas%d" % s, [], ["ADAS%d" % s])
                    for h in range(2):
                        b.mm(PS[6 + h], CACTB[:, kc, :], ADAS[s][:, h * 512:(h + 1) * 512], kc == 0, kc == 15,
                             ["CACTB", "ADAS%d" % s], ["PS%d" % (6 + h)])
                    if kc < 15:
                        yield
                for h in range(2):
                    b.tt("dve", MB[bb][:, h * 512:(h + 1) * 512], PS[6 + h], ADAB[bb][:, h * 512:(h + 1) * 512], ALU.add,
                         ["PS%d" % (6 + h), "ADAB%d" % bb], ["MB%d" % bb])
                if dbg:
                    b.dma("sp", mod_scr[li:li + 1, cg * 1024:(cg + 1) * 1024], MB[bb][0:1, :], "mrow%d" % bb, ["MB%d" % bb], ["modscr%d" % li])
                for jj in range(8):
                    col = cg * 8 + jj
                    b.p.op("dve", lambda e, bb=bb, jj=jj, li=li, col=col: e.scalar_tensor_tensor(
                        out=JUNK, in0=MB[bb][:, jj * 128:(jj + 1) * 128], scalar=1.0, in1=IDENT, op0=ALU.mult, op1=ALU.mult,
                        accum_out=MODF[:, li, col:col + 1]), ["MB%d" % bb, "IDENT"], ["JUNK", "MODF%d" % li])
                yield
            yield

    ada_gen = ada_steps()

    def ada_advance(n):
        for _ in range(n):
            try:
                next(ada_gen)
            except StopIteration:
                return False
        return True

    ada_advance(64)

    if cfg.get("stop_after") == "ada":
        while ada_advance(1):
            pass
        p.barrier()
        p.emit(final_waits=list(p.last_dma.values()))
        return nc
    phaseA(b, arena, PS, dict(IDENT=IDENT, MASK=MASK, PSWM=PSWM, FLAG=FLAG, HALFPI=HALFPI, MAGP=MAGP, MAGN=MAGN, ONEP=ONEP),
           dict(x_own=x_own, x_prev=x_prev, lr2=lr2, li2=li2, ldt=ldt, bri=bri, bir=bir, cri=cri, cir=cir,
                nidx=nidx_d, tau=tau_d, mod_scr=mod_scr, y_scr=y_scr, sel=sel_d, MODF=MODF), ada_advance)
    while ada_advance(1):
        pass
    if cfg.get("phaseA_only", False):
        p.barrier()
        p.emit(final_waits=list(p.last_dma.values()))
        return nc
    p.barrier()
    arena.reset(persist_mark)
    W = dict(glu0=(w_glu, 16, [[j, 16 + j] for j in range(16)]),
             gu0=(w_gu[0], 16, [[j, 44 + j] for j in range(44)]),
             dn0=(w_dn[0], 44, [[j] for j in range(16)]),
             pw1=(w_pw1, 16, [[j, 16 + j] for j in range(16)]),
             pw2=(w_pw2, 16, [[j] for j in range(16)]),
             gu1=(w_gu[1], 16, [[j, 44 + j] for j in range(44)]),
             dn1=(w_dn[1], 44, [[j] for j in range(16)]))
    wscr = phaseW(b, arena, W)
    p.barrier()
    arena.reset(persist_mark)
    outs = phaseB(b, arena, PS, dict(IDENT=IDENT, IDENTB=IDENTB, ONESB=ONESB, FLAG=FLAG, VECS=VECS, CWF=CWF, MODF=MODF),
                  dict(x_own=x_own, x_prev=x_prev, y_scr=y_scr, out=out_d), W, wscr)
    p.barrier()
    p.emit(final_waits=list(p.last_dma.values()))
    return nc


def phaseW(b, arena, W):
    A = arena.alloc
    cfg = b.cfg
    STG = [A([128, 22, 128]) for _ in range(2)]
    WB = [A([128, 4096], BF16) for _ in range(2)]
    wscr = {}
    cnt = 0
    ucnt = 0
    engs = ("act", "pool", "dve")
    for name, (w, KC, units) in W.items():
        nb = len(units[0])
        ks = 2 if KC == 44 else 1
        kcs = KC // ks
        scr = b.dscr("ws_" + name, [len(units) * ks, 128, nb * kcs * 128], BF16)
        wscr[name] = scr
        if cfg.get("skip_w", False):
            continue
        for u, blks in enumerate(units):
            for h in range(ks):
                wb = ucnt % 2
                ucnt += 1
                for bi, cb in enumerate(blks):
                    st = cnt % 2
                    cnt += 1
                    src = w[h * kcs * 128:(h + 1) * kcs * 128, cb * 128:(cb + 1) * 128].rearrange("(kc q) c -> q kc c", q=128)
                    b.dma("sp", STG[st][:, 0:kcs, :], src, "stg%d" % st, [], ["STG%d" % st])
                    dst = WB[wb][:, bi * kcs * 128:(bi + 1) * kcs * 128].rearrange("q (kc c) -> q kc c", c=128)
                    b.copy(engs[cnt % 3], dst, STG[st][:, 0:kcs, :], ["STG%d" % st], ["WB%d" % wb])
                b.dma("sp", scr[u * ks + h], WB[wb][:, 0:nb * kcs * 128], "wb%d" % wb, ["WB%d" % wb], ["ws_%s.%d" % (name, u * ks + h)])
    return wscr


def phaseB(b, arena, PS, C, Dr, W, wscr):
    p = b.p
    cfg = b.cfg
    A = arena.alloc
    IDENT, IDENTB, ONESB, FLAG, VECS, CWF, MODF = C["IDENT"], C["IDENTB"], C["ONESB"], C["FLAG"], C["VECS"], C["CWF"], C["MODF"]
    SCL = A([128, 40, NCH])
    nscl = [0]
    names = {}

    def scl(nm):
        names[nm] = SCL[:, nscl[0], :]
        nscl[0] += 1
        return names[nm]

    mod = lambda li, k: MODF[:, li, k * 16:(k + 1) * 16]
    vec = lambda i: VECS[:, i, :]
    tmp = scl("tmp")
    EPS = A([128, 1])
    b.memset("pool", EPS, LN_EPS, ["EPS"])

    def pt(out, i0, i1, op, rd):
        b.tt("pool", out, i0, i1, op, rd + ["SCL"], ["SCL"])

    def pa(out, i0, val, op, rd):
        b.ts("pool", out, i0, val, None, op, None, rd + ["SCL"], ["SCL"])

    A0 = scl("A0"); pa(A0, mod(0, 1), 1.0, ALU.add, ["MODF0"])
    G1 = {}
    for li in range(2):
        for k, part in ((0, 2), (1, 5)):
            G1[(li, k)] = scl("G1_%d%d" % (li, k))
            pa(G1[(li, k)], mod(li, part), 1.0, ALU.add, ["MODF%d" % li])
    LNS = {}
    for li in range(2):
        for k in range(2):
            g = vec(li * 2 + k); bb = vec(4 + li * 2 + k)
            d = {}
            if (li, k) == (1, 1):
                d["HS"] = g; d["HB"] = bb
            else:
                nli, scp, shp = (li, 4, 3) if k == 0 else (1, 1, 0)
                pa(tmp, mod(nli, scp), 1.0, ALU.add, ["MODF%d" % nli])
                d["HS"] = scl("HS%d%d" % (li, k)); pt(d["HS"], g, tmp, ALU.mult, ["VECS"])
                d["HB"] = scl("HB%d%d" % (li, k)); pt(d["HB"], bb, tmp, ALU.mult, ["VECS"])
                pt(d["HB"], d["HB"], mod(nli, shp), ALU.add, ["MODF%d" % nli])
                d["RS"] = scl("RS%d%d" % (li, k)); pa(d["RS"], g, ALPHA, ALU.mult, ["VECS"])
                d["RB"] = scl("RB%d%d" % (li, k)); pa(d["RB"], bb, ALPHA, ALU.mult, ["VECS"])
            LNS[(li, k)] = d
    B0 = mod(0, 0); DK = vec(8); CBD = vec(9); CG = vec(10); CB = vec(11)

    XR = A([128, NCH, NT]); HB = A([128, NCH, NT], BF16); FH = A([128, NFC, NT], BF16)
    VB = A([128, NCH, 30 + NT], BF16)
    CV = FH[:].rearrange("q a n -> q (a n)")[:, 0:2 * NCH * NT].bitcast(F32).rearrange("q (a n) -> q a n", a=NCH)
    XL = [A([128, 4, 512]) for _ in range(2)]
    SG = [A([128, NT]) for _ in range(2)]; TT = [A([128, NT]) for _ in range(2)]
    RBb = [A([128, NT], BF16) for _ in range(2)]; SQ = [A([128, NT], BF16) for _ in range(2)]
    MEAN = A([128, NT]); VAR = A([128, NT]); RSTD = A([128, NT]); NMR = A([128, NT])
    DG = A([128, CW, 128], BF16)
    WR = [A([128, 4096], BF16) for _ in range(4)]
    print("phase B arena top", arena.top)
    st = dict(ring=0, gbank=0, tcnt=0, pend=[])
    kP = lambda i: "PS%d" % i

    def flush_pending(keep=0):
        while len(st["pend"]) > keep:
            st["pend"].pop(0)()

    def gemm(name, N, rhs, epilogue):
        w, KC, units = W[name]
        nb = len(units[0])
        ks = 2 if KC == 44 else 1
        kcs = KC // ks
        for u in range(len(units)):
            if nb == 2:
                banks = [st["gbank"] * 2, st["gbank"] * 2 + 1]
                st["gbank"] ^= 1
            else:
                banks = [st["gbank"] * 2 + (u % 2)]
                if u % 2 == 1:
                    st["gbank"] ^= 1
            for h in range(ks):
                slot = st["ring"] % 4
                st["ring"] += 1
                b.dma("sp", WR[slot][:, 0:nb * kcs * 128], wscr[name][u * ks + h], "wr%d" % slot,
                      ["ws_%s.%d" % (name, u * ks + h)], ["WR%d" % slot])
                for bi in range(nb):
                    for kc in range(kcs):
                        rap, rkey = rhs(h * kcs + kc)
                        b.mm(PS[banks[bi]][:, 0:N], WR[slot][:, (bi * kcs + kc) * 128:(bi * kcs + kc + 1) * 128], rap,
                             h == 0 and kc == 0, h == ks - 1 and kc == kcs - 1, ["WR%d" % slot, rkey], [kP(banks[bi])])
            flush_pending(keep=0)
            epilogue(u, banks)
        flush_pending()

    def ln_stats(j, src, key, N):
        s = st["tcnt"] % 2
        b.act(SQ[s][:, 0:N], src, AF.Square, [key], ["SQ%d" % s])
        b.copy("pool", RBb[s][:, 0:N], src, [key], ["RBb%d" % s])

        def pe_part(j=j, s=s, N=N):
            b.mm(PS[4][:, 0:N], ONESB, RBb[s][:, 0:N], j == 0, j == NCH - 1, ["ONESB", "RBb%d" % s], [kP(4)])
            b.mm(PS[5][:, 0:N], ONESB, SQ[s][:, 0:N], j == 0, j == NCH - 1, ["ONESB", "SQ%d" % s], [kP(5)])
        st["pend"].append(pe_part)
        st["tcnt"] += 1

    def ln_finish(N):
        flush_pending()
        b.copy("act", MEAN[:, 0:N], PS[4][:, 0:N], [kP(4)], ["MEAN"])
        b.act(VAR[:, 0:N], PS[4][:, 0:N], AF.Square, [kP(4)], ["VAR"])
        b.tt("dve", VAR[:, 0:N], PS[5][:, 0:N], VAR[:, 0:N], ALU.subtract, [kP(5), "VAR"], ["VAR"])
        b.act(RSTD[:, 0:N], VAR[:, 0:N], AF.Sqrt, ["VAR", "EPS"], ["RSTD"], bias=EPS[:, 0:1])
        b.p.op("dve", lambda e: e.reciprocal(out=RSTD[:, 0:N], in_=RSTD[:, 0:N]), ["RSTD"], ["RSTD"])
        b.stt(NMR[:, 0:N], MEAN[:, 0:N], -1.0, RSTD[:, 0:N], ALU.mult, ALU.mult, ["MEAN", "RSTD"], ["NMR"])

    def ln_norm(j, src, key, N):
        s = st["tcnt"] % 2
        st["tcnt"] += 1
        b.tt("dve", TT[s][:, 0:N], src, RSTD[:, 0:N], ALU.mult, [key, "RSTD"], ["TT%d" % s])
        b.tt("pool", TT[s][:, 0:N], TT[s][:, 0:N], NMR[:, 0:N], ALU.add, ["TT%d" % s, "NMR"], ["TT%d" % s])
        return TT[s][:, 0:N], "TT%d" % s

    def ln_apply_std(lk, N):
        d = LNS[lk]
        ln_finish(N)
        for j in range(NCH):
            t, tk = ln_norm(j, XR[:, j, 0:N], "XR.%d" % j, N)
            if lk == (1, 1):
                b.act(XR[:, j, 0:N], t, AF.Identity, [tk, "SCL", "VECS"], ["XR.%d" % j], scale=d["HS"][:, j:j + 1], bias=d["HB"][:, j:j + 1])
            else:
                b.act(HB[:, j, 0:N], t, AF.Identity, [tk, "SCL"], ["HB.%d" % j], scale=d["HS"][:, j:j + 1], bias=d["HB"][:, j:j + 1])
                b.ts("pool", XR[:, j, 0:N], t, d["RS"][:, j:j + 1], d["RB"][:, j:j + 1], ALU.mult, ALU.add, [tk, "SCL"], ["XR.%d" % j])

    def run_tile(xrows, yrows, orows, N, halo):
        NBk = N // 128
        hb_rhs = lambda kc: (HB[:, kc, 0:N], "HB.%d" % kc)
        fh_rhs = lambda kc: (FH[:, kc, 0:N], "FH.%d" % kc)
        for jg in range(4):
            b.dma("pool", XL[0][:, 0:NBk, :], xrows[:, jg * 512:(jg + 1) * 512].rearrange("(k t) c -> t k c", t=128), "xl0", [], ["XL0"])
            b.dma("pool", XL[1][:, 0:NBk, :], yrows[:, jg * 512:(jg + 1) * 512].rearrange("(k t) c -> t k c", t=128), "xl1", ["yscr"], ["XL1"])
            for jl in range(4):
                j = jg * 4 + jl
                s = st["tcnt"] % 2
                st["tcnt"] += 1
                for k in range(NBk):
                    b.tr(PS[6][:, k * 128:(k + 1) * 128], XL[0][:, k, jl * 128:(jl + 1) * 128], IDENT, ["XL0", "IDENT"], [kP(6)])
                b.act(SG[s][:, 0:N], PS[6][:, 0:N], AF.Identity, [kP(6), "SCL", "MODF0"], ["SG%d" % s], scale=A0[:, j:j + 1], bias=B0[:, j:j + 1])
                b.act(XR[:, j, 0:N], PS[6][:, 0:N], AF.Identity, [kP(6)], ["XR.%d" % j], scale=ALPHA)
                for k in range(NBk):
                    b.tr(PS[7][:, k * 128:(k + 1) * 128], XL[1][:, k, jl * 128:(jl + 1) * 128], IDENT, ["XL1", "IDENT"], [kP(7)])
                b.stt(TT[s][:, 0:N], SG[s][:, 0:N], DK[:, j:j + 1], PS[7][:, 0:N], ALU.mult, ALU.add, ["SG%d" % s, "VECS", kP(7)], ["TT%d" % s])
                b.act(HB[:, j, 0:N], TT[s][:, 0:N], AF.Gelu_apprx_tanh, ["TT%d" % s], ["HB.%d" % j])

        def epi_glu(g1):
            def f(u, banks):
                s = st["tcnt"] % 2
                st["tcnt"] += 1
                b.act(SG[s][:, 0:N], PS[banks[1]][:, 0:N], AF.Sigmoid, [kP(banks[1])], ["SG%d" % s])
                b.tt("dve", TT[s][:, 0:N], PS[banks[0]][:, 0:N], SG[s][:, 0:N], ALU.mult, [kP(banks[0]), "SG%d" % s], ["TT%d" % s])
                b.stt(XR[:, u, 0:N], TT[s][:, 0:N], g1[:, u:u + 1], XR[:, u, 0:N], ALU.mult, ALU.add, ["TT%d" % s, "SCL", "XR.%d" % u], ["XR.%d" % u])
                ln_stats(u, XR[:, u, 0:N], "XR.%d" % u, N)
            return f

        def epi_ffn(u, banks):
            s = st["tcnt"] % 2
            st["tcnt"] += 1
            b.act(SG[s][:, 0:N], PS[banks[0]][:, 0:N], AF.Silu, [kP(banks[0])], ["SG%d" % s])
            b.tt("dve", FH[:, u, 0:N], PS[banks[1]][:, 0:N], SG[s][:, 0:N], ALU.mult, [kP(banks[1]), "SG%d" % s], ["FH.%d" % u])

        def epi_res(g1):
            def f(u, banks):
                b.stt(XR[:, u, 0:N], PS[banks[0]][:, 0:N], g1[:, u:u + 1], XR[:, u, 0:N], ALU.mult, ALU.add, [kP(banks[0]), "SCL", "XR.%d" % u], ["XR.%d" % u])
                ln_stats(u, XR[:, u, 0:N], "XR.%d" % u, N)
            return f

        def epi_pw1(u, banks):
            s = st["tcnt"] % 2
            st["tcnt"] += 1
            b.act(SG[s][:, 0:N], PS[banks[1]][:, 0:N], AF.Sigmoid, [kP(banks[1])], ["SG%d" % s])
            b.tt("dve", VB[:, u, 30:30 + N], PS[banks[0]][:, 0:N], SG[s][:, 0:N], ALU.mult, [kP(banks[0]), "SG%d" % s], ["VB.%d" % u])

        gemm("glu0", N, hb_rhs, epi_glu(G1[(0, 0)]))
        ln_apply_std((0, 0), N)
        gemm("gu0", N, hb_rhs, epi_ffn)
        gemm("dn0", N, fh_rhs, epi_res(G1[(0, 1)]))
        ln_apply_std((0, 1), N)
        gemm("pw1", N, hb_rhs, epi_pw1)
        if halo:
            for j in range(NCH):
                b.ts("pool", VB[:, j, 0:30], VB[:, j, N:N + 30], FLAG[:, 0:1], None, ALU.mult, None, ["VB.%d" % j, "FLAG"], ["VB.%d" % j])
            return
        for j in range(NCH):
            b.tt("pool", DG, IDENTB[:].unsqueeze(1).to_broadcast([128, CW, 128]), CWF[:, j, :].unsqueeze(2).to_broadcast([128, CW, 128]),
                 ALU.mult, ["IDENTB", "CWF"], ["DG"])
            bank = st["gbank"] * 2 + (j % 2)
            if j % 2 == 1:
                st["gbank"] ^= 1
            for k in range(CW):
                b.mm(PS[bank][:, 0:N], DG[:, k, :], VB[:, j, k:k + N], k == 0, k == CW - 1, ["DG", "VB.%d" % j], [kP(bank)])
            flush_pending()
            b.act(CV[:, j, 0:N], PS[bank][:, 0:N], AF.Identity, [kP(bank), "VECS"], ["FH.%d" % (2 * j), "FH.%d" % (2 * j + 1)], bias=CBD[:, j:j + 1])
            ln_stats(j, CV[:, j, 0:N], "FH.%d" % (2 * j), N)
            b.copy("pool", VB[:, j, 0:30], VB[:, j, N:N + 30], ["VB.%d" % j], ["VB.%d" % j])
        ln_finish(N)
        for j in range(NCH):
            t, tk = ln_norm(j, CV[:, j, 0:N], "FH.%d" % (2 * j), N)
            b.act(HB[:, j, 0:N], t, AF.Silu, [tk, "VECS"], ["HB.%d" % j], scale=CG[:, j:j + 1], bias=CB[:, j:j + 1])
        gemm("pw2", N, hb_rhs, epi_res(G1[(1, 0)]))
        ln_apply_std((1, 0), N)
        gemm("gu1", N, hb_rhs, epi_ffn)
        gemm("dn1", N, fh_rhs, epi_res(G1[(1, 1)]))
        ln_apply_std((1, 1), N)
        for jg in range(4):
            for k in range(NBk):
                bank = 6 + (k % 2)
                for jl in range(4):
                    j = jg * 4 + jl
                    b.tr(PS[bank][:, jl * 128:(jl + 1) * 128], XR[:, j, k * 128:(k + 1) * 128], IDENT, ["XR.%d" % j, "IDENT"], [kP(bank)])
                b.copy("act" if k % 2 == 0 else "dve", XL[0][:, k, :], PS[bank], [kP(bank)], ["XL0"])
            b.dma("pool", orows[:, jg * 512:(jg + 1) * 512].rearrange("(k t) c -> t k c", t=128), XL[0][:, 0:NBk, :], "outs", ["XL0"], ["out"])

    y_scr = Dr["y_scr"]
    run_tile(Dr["x_prev"][SEQH - NH:SEQH, :], y_scr[0:NH, :], None, NH, True)
    n_tiles = cfg.get("n_tiles", SEQH // NT)
    for t in range(n_tiles):
        run_tile(Dr["x_own"][t * NT:(t + 1) * NT, :], y_scr[NH + t * NT:NH + (t + 1) * NT, :], Dr["out"][t * NT:(t + 1) * NT, :], NT, False)


def phaseA(b, arena, PS, C, Dr, ada_advance):
    p = b.p
    A = arena.alloc
    IDENT, MASK, PSWM, FLAG = C["IDENT"], C["MASK"], C["PSWM"], C["FLAG"]
    cfg = b.cfg
    n_slabs = cfg.get("n_slabs", 16)
    LR2 = A([128, G]); LI2 = A([128, G]); DT = A([128, G]); AR = A([128, G]); TH = A([128, G])
    NIDX = A([128, 512])
    E_R = A([128, 16, G]); E_I = A([128, 16, G]); E_Rs = A([128, 16, G]); E_Is = A([128, 16, G])
    QR = A([128, G]); QI = A([128, G]); QIs = A([128, G])
    PHIRED = A([128, G]); PHI2PI = A([128, G]); RHO = A([128, G]); C512 = A([128, G]); S512 = A([128, G])
    SC1 = A([128, G]); SHS = A([128, G]); GEND = A([128, G]); CARRY = A([128, G])
    tm = [A([128, G]) for _ in range(6)]
    mk = arena.mark()
    TAU = A([128, 16, G]); T0 = A([128, 16, G]); T1 = A([128, 16, G]); T2b = A([128, 16, G])
    for (dst, src, k) in [(LR2, Dr["lr2"], "LR2"), (LI2, Dr["li2"], "LI2"), (DT, Dr["ldt"], "DT"), (NIDX, Dr["nidx"], "NIDX"), (TAU, Dr["tau"], "TAU")]:
        b.dma("sp", dst, src, None, [], [k])
    b.act(DT, DT, AF.Exp, ["DT"], ["DT"])
    b.tt("pool", AR, LR2, DT, ALU.mult, ["LR2", "DT"], ["AR"])
    b.tt("pool", TH, LI2, DT, ALU.mult, ["LI2", "DT"], ["TH"])
    bc = lambda t: t[:].unsqueeze(1).to_broadcast([128, 16, G])
    b.tt("pool", T0, TAU, bc(TH), ALU.mult, ["TAU", "TH"], ["T0"])
    b.tt("pool", E_Rs, TAU, bc(AR), ALU.mult, ["TAU", "AR"], ["E_Rs"])
    b.act(E_Rs, E_Rs, AF.Exp, ["E_Rs"], ["E_Rs"])
    b.sincos(T0, T1, T2b, E_I, E_R, "T0", "T1", "T2b", "E_I", "E_R")
    b.tt("pool", E_R, E_R, E_Rs, ALU.mult, ["E_R", "E_Rs"], ["E_R"])
    b.tt("pool", E_I, E_I, E_Rs, ALU.mult, ["E_I", "E_Rs"], ["E_I"])
    b.copy("pool", E_Rs, E_R, ["E_R"], ["E_Rs"])
    b.copy("pool", E_Is, E_I, ["E_I"], ["E_Is"])
    b.ts("pool", E_Rs[64:128], E_Rs[64:128], -1.0, None, ALU.mult, None, ["E_Rs"], ["E_Rs"])
    b.ts("pool", E_Is[0:64], E_Is[0:64], -1.0, None, ALU.mult, None, ["E_Is"], ["E_Is"])
    t_nr, t_den, t_a, t_b, t_phi, t_k = tm
    b.ts("pool", t_nr, E_R[:, 9, :], -1.0, None, ALU.add, None, ["E_R"], ["t_nr"])
    b.tt("pool", t_den, LR2, LR2, ALU.mult, ["LR2"], ["t_den"])
    b.tt("pool", t_a, LI2, LI2, ALU.mult, ["LI2"], ["t_a"])
    b.tt("pool", t_den, t_den, t_a, ALU.add, ["t_den", "t_a"], ["t_den"])
    b.p.op("dve", lambda e: e.reciprocal(out=t_den, in_=t_den), ["t_den"], ["t_den"])
    b.tt("pool", t_a, t_nr, LR2, ALU.mult, ["t_nr", "LR2"], ["t_a"])
    b.tt("pool", t_b, E_I[:, 9, :], LI2, ALU.mult, ["E_I", "LI2"], ["t_b"])
    b.tt("pool", t_a, t_a, t_b, ALU.add, ["t_a", "t_b"], ["t_a"])
    b.tt("pool", QR, t_a, t_den, ALU.mult, ["t_a", "t_den"], ["QR"])
    b.tt("pool", t_a, E_I[:, 9, :], LR2, ALU.mult, ["E_I", "LR2"], ["t_a"])
    b.tt("pool", t_b, t_nr, LI2, ALU.mult, ["t_nr", "LI2"], ["t_b"])
    b.tt("pool", t_a, t_a, t_b, ALU.subtract, ["t_a", "t_b"], ["t_a"])
    b.tt("pool", QI, t_a, t_den, ALU.mult, ["t_a", "t_den"], ["QI"])
    b.copy("pool", QIs, QI, ["QI"], ["QIs"])
    b.ts("pool", QIs[0:64], QIs[0:64], -1.0, None, ALU.mult, None, ["QIs"], ["QIs"])
    b.ts("pool", t_phi, TH, 8.0, None, ALU.mult, None, ["TH"], ["t_phi"])
    b.ts("dve", t_k, t_phi, 1.0 / TWO_PI, MAGIC, ALU.mult, ALU.add, ["t_phi"], ["t_k"])
    b.ts("dve", t_k, t_k, -MAGIC, None, ALU.add, None, ["t_k"], ["t_k"])
    b.stt(PHIRED, t_k, -CW1, t_phi, ALU.mult, ALU.add, ["t_k", "t_phi"], ["PHIRED"])
    b.stt(PHIRED, t_k, -CW2, PHIRED, ALU.mult, ALU.add, ["t_k", "PHIRED"], ["PHIRED"])
    b.stt(PHIRED, t_k, -CW3, PHIRED, ALU.mult, ALU.add, ["t_k", "PHIRED"], ["PHIRED"])
    b.ts("pool", PHI2PI, PHIRED, 1.0 / TWO_PI, None, ALU.mult, None, ["PHIRED"], ["PHI2PI"])
    b.act(RHO, AR, AF.Exp, ["AR"], ["RHO"], scale=8.0)
    b.ts("pool", t_phi, PHIRED, 512.0, None, ALU.mult, None, ["PHIRED", "t_k"], ["t_phi"])
    b.sincos(t_phi, t_k, t_a, S512, C512, "t_phi", "t_k", "t_a", "S512", "C512")
    SEL = T0
    MODF = Dr["MODF"]
    b.dma("sp", SEL[:, 0:8, :], Dr["sel"], None, ["T0"], ["T0"])
    for gm in range(8):
        b.mm(PS[4][:, gm * 32:(gm + 1) * 32], SEL[:, gm, :], MODF[:, 0, 0:32], True, True, ["T0", "MODF0"], ["PS4"])
    pv = PS[4][:, 0:256].rearrange("p (gm c) -> p gm c", gm=8)
    b.copy("act", SHS[:].rearrange("p (gj gm) -> p gm gj", gm=8), pv[:, :, 0:16], ["PS4"], ["SHS"])
    b.act(SC1[:].rearrange("p (gj gm) -> p gm gj", gm=8), pv[:, :, 16:32], AF.Identity, ["PS4"], ["SC1"], bias=C["ONEP"][:, 0:1])
    b.memset("pool", GEND, 0.0, ["GEND"])
    b.memset("pool", CARRY, 0.0, ["CARRY"])
    p.barrier()
    arena.reset(mk)
    if cfg.get("stop_after") == "setup":
        return

    X2 = A([128, 4, 8, 128]); XG = A([128, 4, 8, 128]); YG = A([128, 4, 8, 128]); YGH = YG[:, 0]
    BS = [[A([128, 8, 16]) for _ in range(4)] for _ in range(2)]
    BB1 = A([128, 8, 16]); BB2 = A([128, 8, 16]); TB = A([128, 8, 16])
    LT1 = A([128, 8, 128]); LT2 = A([128, 8, 128]); WC1 = A([128, 8, 128]); WC2 = A([128, 8, 128]); TS = A([128, 8, 128])
    WS1 = [A([128, 128]) for _ in range(2)]; WS2 = [A([128, 128]) for _ in range(2)]; WK = [A([128, 128]) for _ in range(2)]
    COS = [A([128, 512]) for _ in range(2)]; SIN = [A([128, 512]) for _ in range(2)]
    ANG = A([128, 512]); KB = A([128, 512]); RB = A([128, 512])
    U = [A([128, 512]) for _ in range(2)]
    T2 = A([128, 512])
    GP = [A([128, 513]) for _ in range(2)]
    Fb = [A([128, 512]) for _ in range(2)]
    FC = A([128, 512]); FS = A([128, 512]); YSB = A([128, 512])
    print("phase A arena top", arena.top)
    PSU, PSS1, PSS2, PSY, PSYT, PSWK = 0, 1, 2, 3, 4, 5
    y_scr = Dr["y_scr"]
    bsrc = (Dr["bri"], Dr["bir"], Dr["cri"], Dr["cir"])
    bnames = ("bri", "bir", "cri", "cir")

    def ebc(T, lo, g0):
        return T[:, lo:lo + 8, g0:g0 + 8].transpose([0, 2, 1]).unsqueeze(3).to_broadcast([128, 8, 8, 16])

    def bbc(T):
        return T[:].unsqueeze(2).to_broadcast([128, 8, 8, 16])

    v4 = lambda T: T[:].rearrange("p g (s i) -> p g s i", s=8)
    seq = [(ps_, sl) for ps_ in ("pre", "own") for sl in range(n_slabs)]

    def load_slab(idx):
        ps_, sl = seq[idx]
        xsrc = Dr["x_prev"] if ps_ == "pre" else Dr["x_own"]
        g0 = sl * 8
        for t, src, k in zip(BS[idx % 2], bsrc, bnames):
            b.dma("sp", t, src[:, g0:g0 + 8, :], None, [], ["BS%d.%s" % (idx % 2, k)])
        for q in range(4):
            b.dma("sp", X2[:, q], xsrc[q * 1024:(q + 1) * 1024, g0 * 16:(g0 + 8) * 16].rearrange("(c s) f -> c s f", s=8),
                  None, [], ["X2.%d" % q])

    load_slab(0)
    gi = 0
    for idx, (ps_, sl) in enumerate(seq):
        g0 = sl * 8
        bs = BS[idx % 2]
        kb = lambda k: "BS%d.%s" % (idx % 2, k)
        for q in range(4):
            eng = "act" if q % 2 == 0 else "pool"
            src = X2[:, q].rearrange("p s (g i) -> p g s i", g=8)
            dst = XG[:, q].rearrange("p g (s i) -> p g s i", s=8)
            b.copy(eng, dst, src, ["X2.%d" % q], ["XG.%d" % q])
        if idx + 1 < len(seq):
            load_slab(idx + 1)
        qb = lambda T: T[:, g0:g0 + 8].unsqueeze(2).to_broadcast([128, 8, 16])
        b.tt("pool", BB1, qb(QR), bs[0], ALU.mult, ["QR", kb("bri")], ["BB1"])
        b.tt("pool", TB, qb(QIs), bs[1], ALU.mult, ["QIs", kb("bir")], ["TB"])
        b.tt("pool", BB1, BB1, TB, ALU.add, ["BB1", "TB"], ["BB1"])
        b.tt("pool", BB2, qb(QR), bs[1], ALU.mult, ["QR", kb("bir")], ["BB2"])
        b.tt("pool", TB, qb(QIs), bs[0], ALU.mult, ["QIs", kb("bri")], ["TB"])
        b.tt("pool", BB2, BB2, TB, ALU.subtract, ["BB2", "TB"], ["BB2"])
        b.tt("pool", v4(LT1), ebc(E_R, 0, g0), bbc(BB1), ALU.mult, ["E_R", "BB1"], ["LT1"])
        b.tt("pool", v4(TS), ebc(E_Is, 0, g0), bbc(BB2), ALU.mult, ["E_Is", "BB2"], ["TS"])
        b.tt("pool", LT1, LT1, TS, ALU.add, ["LT1", "TS"], ["LT1"])
        b.tt("pool", v4(LT2), ebc(E_Rs, 0, g0), bbc(BB2), ALU.mult, ["E_Rs", "BB2"], ["LT2"])
        b.tt("pool", v4(TS), ebc(E_I, 0, g0), bbc(BB1), ALU.mult, ["E_I", "BB1"], ["TS"])
        b.tt("pool", LT2, LT2, TS, ALU.add, ["LT2", "TS"], ["LT2"])
        b.tt("pool", v4(WC1), ebc(E_Rs, 8, g0), bbc(bs[2]), ALU.mult, ["E_Rs", kb("cri")], ["WC1"])
        b.tt("pool", v4(TS), ebc(E_I, 8, g0), bbc(bs[3]), ALU.mult, ["E_I", kb("cir")], ["TS"])
        b.tt("pool", WC1, WC1, TS, ALU.subtract, ["WC1", "TS"], ["WC1"])
        b.tt("pool", v4(WC2), ebc(E_Is, 8, g0), bbc(bs[2]), ALU.mult, ["E_Is", kb("cri")], ["WC2"])
        b.tt("pool", v4(TS), ebc(E_R, 8, g0), bbc(bs[3]), ALU.mult, ["E_R", kb("cir")], ["TS"])
        b.tt("pool", WC2, WC2, TS, ALU.subtract, ["WC2", "TS"], ["WC2"])
        for gl in range(8):
            Gi = g0 + gl
            pb = gi % 2
            gi += 1
            ada_advance(1)
            kP = lambda i: "PS%d" % i
            stage = cfg.get("grp_stage", 99)
            if stage < 1:
                continue
            dm = cfg.get("dbgmask", 15)
            if dm & 1:
                b.tr(PS[PSWK][:, 0:128], LT1[:, gl, :], IDENT, ["LT1", "IDENT"], [kP(PSWK)])
                b.tr(PS[PSWK][:, 128:256], LT2[:, gl, :], IDENT, ["LT2", "IDENT"], [kP(PSWK)])
            if dm & 2:
                b.mm(PS[PSWK][:, 256:384], LT1[:, gl, :], WC1[:, gl, :], True, True, ["LT1", "WC1"], [kP(PSWK)])
            if dm & 4:
                b.copy("act", WS1[pb], PS[PSWK][:, 0:128], [kP(PSWK)], ["WS1.%d" % pb])
                b.copy("act", WS2[pb], PS[PSWK][:, 128:256], [kP(PSWK)], ["WS2.%d" % pb])
            if dm & 8:
                b.tt("dve", WK[pb], PS[PSWK][:, 256:384], MASK, ALU.mult, [kP(PSWK), "MASK"], ["WK.%d" % pb])
            if stage < 2:
                continue
            b.act(ANG, NIDX, AF.Identity, ["NIDX", "PHIRED"], ["ANG"], scale=PHIRED[:, Gi:Gi + 1])
            b.act(KB, NIDX, AF.Identity, ["NIDX", "PHI2PI"], ["KB"], scale=PHI2PI[:, Gi:Gi + 1], bias=C["MAGP"][:, 0:1])
            b.act(KB, KB, AF.Identity, ["KB"], ["KB"], bias=C["MAGN"][:, 0:1])
            b.stt(RB, KB, -CW1, ANG, ALU.mult, ALU.add, ["KB", "ANG"], ["RB"])
            b.stt(RB, KB, -CW2, RB, ALU.mult, ALU.add, ["KB", "RB"], ["RB"])
            b.act(SIN[pb], RB, AF.Sin, ["RB"], ["SIN.%d" % pb], scale=SIN_SCALE)
            b.act(KB, RB, AF.Abs, ["RB"], ["KB"])
            b.act(COS[pb], KB, AF.Sin, ["KB"], ["COS.%d" % pb], scale=-1.0, bias=C["HALFPI"][:, 0:1])
            if stage < 3:
                continue
            for q in range(4):
                b.tr(PS[PSU][:, q * 128:(q + 1) * 128], XG[:, q, gl, :], IDENT, ["XG.%d" % q, "IDENT"], [kP(PSU)])
            b.act(U[pb], PS[PSU], AF.Identity, [kP(PSU), "SC1", "SHS"], ["U.%d" % pb],
                  scale=SC1[:, Gi:Gi + 1], bias=SHS[:, Gi:Gi + 1])
            if stage < 4:
                continue
            b.mm(PS[PSS1], WS1[pb], U[pb], True, True, ["WS1.%d" % pb, "U.%d" % pb], [kP(PSS1)])
            b.mm(PS[PSS2], WS2[pb], U[pb], True, True, ["WS2.%d" % pb, "U.%d" % pb], [kP(PSS2)])
            b.tt("dve", GP[pb][:, 1:513], PS[PSS1], COS[pb], ALU.mult, [kP(PSS1), "COS.%d" % pb], ["GP.%d" % pb])
            b.tt("dve", T2, PS[PSS2], SIN[pb], ALU.mult, [kP(PSS2), "SIN.%d" % pb], ["T2"])
            b.tt("pool", GP[pb][:, 1:513], GP[pb][:, 1:513], T2, ALU.add, ["GP.%d" % pb, "T2"], ["GP.%d" % pb])
            if ps_ == "pre":
                b.memset("pool", GP[pb][:, 0:1], 0.0, ["GP.%d" % pb])
            else:
                b.copy("pool", GP[pb][:, 0:1], CARRY[:, Gi:Gi + 1], ["CARRY", "GP.%d" % pb], ["GP.%d" % pb])
            b.p.op("dve", lambda e, pb=pb, Gi=Gi: e.tensor_tensor_scan(
                out=Fb[pb], data0=GP[pb][:, 0:512], data1=RHO[:, Gi:Gi + 1].to_broadcast([128, 512]),
                initial=0.0, op0=ALU.add, op1=ALU.mult), ["GP.%d" % pb, "RHO"], ["F.%d" % pb])
            if ps_ == "pre":
                b.tt("pool", GEND[:, Gi:Gi + 1], Fb[pb][:, 511:512], GP[pb][:, 512:513], ALU.add, ["F.%d" % pb, "GP.%d" % pb], ["GEND"])
                c0 = 384
            else:
                c0 = 0
            if stage < 5:
                continue
            b.tt("pool", FC[:, c0:512], Fb[pb][:, c0:512], COS[pb][:, c0:512], ALU.mult, ["F.%d" % pb, "COS.%d" % pb], ["FC"])
            b.tt("pool", FS[:, c0:512], Fb[pb][:, c0:512], SIN[pb][:, c0:512], ALU.mult, ["F.%d" % pb, "SIN.%d" % pb], ["FS"])
            b.mm(PS[PSY][:, c0:512], WK[pb], U[pb][:, c0:512], True, False, ["WK.%d" % pb, "U.%d" % pb], [kP(PSY)])
            b.mm(PS[PSY][:, c0:512], WC1[:, gl, :], FC[:, c0:512], False, False, ["WC1", "FC"], [kP(PSY)])
            b.mm(PS[PSY][:, c0:512], WC2[:, gl, :], FS[:, c0:512], False, True, ["WC2", "FS"], [kP(PSY)])
            b.copy("act", YSB[:, c0:512], PS[PSY][:, c0:512], [kP(PSY)], ["YSB"])
            if stage < 6:
                continue
            if ps_ == "own":
                for q in range(4):
                    b.tr(PS[PSYT][:, q * 128:(q + 1) * 128], YSB[:, q * 128:(q + 1) * 128], IDENT, ["YSB", "IDENT"], [kP(PSYT)])
                src = PS[PSYT][:].rearrange("p (q t o) -> p q t o", q=4, t=8)
                b.copy("act", YG[:, :, :, gl * 16:(gl + 1) * 16], src, [kP(PSYT)], ["YG"])
            else:
                b.tr(PS[PSYT][:, 0:128], YSB[:, 384:512], IDENT, ["YSB", "IDENT"], [kP(PSYT)])
                src = PS[PSYT][:, 0:128].rearrange("p (t o) -> p t o", t=8)
                b.copy("act", YGH[:, :, gl * 16:(gl + 1) * 16], src, [kP(PSYT)], ["YG"])
        if cfg.get("grp_stage", 99) < 7:
            pass
        elif ps_ == "own":
            for q in range(4):
                b.dma("pool", y_scr[NH + q * 1024:NH + (q + 1) * 1024, g0 * 16:(g0 + 8) * 16].rearrange("(c t) f -> c t f", t=8),
                      YG[:, q], "yg", ["YG"], ["yscr"])
        else:
            b.dma("pool", y_scr[0:NH, g0 * 16:(g0 + 8) * 16].rearrange("(c t) f -> c t f", t=8), YGH[112:128], "yg", ["YG"], ["yscr"])
        if ps_ == "pre" and sl == n_slabs - 1:
            b.mm(PS[PSWK][:, 0:128], PSWM, GEND, True, True, ["PSWM", "GEND"], ["PS%d" % PSWK])
            b.tt("pool", CARRY, GEND, C512, ALU.mult, ["GEND", "C512"], ["CARRY"])
            b.tt("dve", T2[:, 0:128], PS[PSWK][:, 0:128], S512, ALU.mult, ["PS%d" % PSWK, "S512"], ["T2"])
            b.tt("pool", CARRY, CARRY, T2[:, 0:128], ALU.subtract, ["CARRY", "T2"], ["CARRY"])
            b.ts("pool", CARRY, CARRY, FLAG[:, 0:1], None, ALU.mult, None, ["CARRY", "FLAG"], ["CARRY"])


def _fm(v):
    return np.ascontiguousarray(np.asarray(v, np.float32).reshape(NCH, 128).T)


def prep_common(inp):
    f = np.float32
    o = {}
    lam_re = np.asarray(inp["s5_lam_re"][0], f); lam_im = np.asarray(inp["s5_lam_im"][0], f)
    o["lr2"] = np.ascontiguousarray(np.concatenate([lam_re.T, lam_re.T], 0))
    o["li2"] = np.ascontiguousarray(np.concatenate([lam_im.T, lam_im.T], 0))
    o["ldt"] = np.ascontiguousarray(np.broadcast_to(np.asarray(inp["s5_log_dt"][0], f)[None, :], (128, G)))
    brT = np.asarray(inp["s5_b_re"][0], f).transpose(1, 0, 2); biT = np.asarray(inp["s5_b_im"][0], f).transpose(1, 0, 2)
    o["bri"] = np.ascontiguousarray(np.concatenate([brT, biT], 0)); o["bir"] = np.ascontiguousarray(np.concatenate([biT, brT], 0))
    crT = np.asarray(inp["s5_c_re"][0], f).transpose(2, 0, 1); ciT = np.asarray(inp["s5_c_im"][0], f).transpose(2, 0, 1)
    o["cri"] = np.ascontiguousarray(np.concatenate([crT, ciT], 0)); o["cir"] = np.ascontiguousarray(np.concatenate([ciT, crT], 0))
    vs = [inp["ln_g"][0, 0], inp["ln_g"][0, 1], inp["ln_g"][1, 0], inp["ln_g"][1, 1],
          inp["ln_b"][0, 0], inp["ln_b"][0, 1], inp["ln_b"][1, 0], inp["ln_b"][1, 1],
          inp["s5_d"][0], inp["cv_b_dw"][0], inp["cv_norm_g"][0], inp["cv_norm_b"][0]]
    o["vecs"] = np.ascontiguousarray(np.stack([_fm(v) for v in vs], 1))
    wdw = np.asarray(inp["cv_w_dw"][0], f)
    o["cw_fm"] = np.ascontiguousarray(wdw.T.reshape(NCH, 128, CW).transpose(1, 0, 2))
    o["ident"] = np.eye(128, dtype=f)
    sel = np.zeros((128, 8, 128), f)
    for gm in range(8):
        for si in range(128):
            sel[gm * 16 + si % 16, gm, si] = 1.0
    o["sel"] = sel
    si = np.arange(128) // 16
    o["mask"] = (si[None, :] >= si[:, None]).astype(f)
    psw = np.zeros((128, 128), f)
    for m in range(64):
        psw[m + 64, m] = 1.0
        psw[m, m + 64] = -1.0
    o["psw"] = psw
    o["nidx"] = np.ascontiguousarray(np.broadcast_to(np.arange(512, dtype=f)[None, :], (128, 512)))
    tau = np.concatenate([-np.arange(8, dtype=f), np.arange(8, dtype=f)])
    o["tau"] = np.ascontiguousarray(np.broadcast_to(tau[None, :, None], (128, 16, G)))
    o["ada_w"] = np.asarray(inp["ada_w"], f); o["ada_b"] = np.asarray(inp["ada_b"], f)
    o["w_glu"] = np.asarray(inp["s5_w_glu"][0], f); o["w_pw1"] = np.asarray(inp["cv_w_pw1"][0], f)
    o["w_pw2"] = np.asarray(inp["cv_w_pw2"][0], f)
    o["w_gu"] = np.asarray(inp["ffn_w_gu"], f); o["w_dn"] = np.asarray(inp["ffn_w_down"], f)
    return o


def core_inputs(inp, common, core):
    bi, half = core // 2, core % 2
    x = np.asarray(inp["x"], np.float32)
    m = dict(common)
    m["x_own"] = np.ascontiguousarray(x[bi, half * SEQH:(half + 1) * SEQH])
    m["x_prev"] = np.ascontiguousarray(x[bi, 0:SEQH]) if half == 1 else np.zeros((SEQH, D), np.float32)
    m["c_fm"] = _fm(inp["c"][bi])
    m["flag"] = np.full((128, 1), float(half), np.float32)
    return m


_CACHE = {}


def run(inp, cfg, cores):
    key = tuple(sorted(cfg.items()))
    if key not in _CACHE:
        _CACHE[key] = build(cfg)
    nc = _CACHE[key]
    common = prep_common(inp)
    in_maps = [core_inputs(inp, common, c) for c in cores]
    return run_bass_kernel_spmd(nc, in_maps, core_ids=list(range(len(cores))))


def kernel(**inputs):
    res = run(inputs, {}, list(range(8)))
    out = np.zeros((4, 2 * SEQH, D), np.float32)
    for c in range(8):
        out[c // 2, (c % 2) * SEQH:(c % 2 + 1) * SEQH] = res.results[c]["out"]
    return out
```
